# Optimizing a Trainium2 kernel written in Bass

```python
import numpy as np
import jax
import jax.numpy as jnp
from jax import lax

D_MODEL = 2048
BATCH = 4
SEQ = 2048
DEPTH = 4
DEC_BATCH = 16
DEC_SEQ = 2048
PAST_LEN = 128

GRID_W = 64
A_HEADS = 16
A_HEAD_DIM = 64
A_WIDTH = A_HEADS * A_HEAD_DIM
A_DECAY_LORA = 64
A_ICLR_LORA = 64
A_GATE_LORA = 160
B_GROUPS = 8
B_GROUP_DIM = 64
B_WIDTH = B_GROUPS * B_GROUP_DIM
B_CHUNK = 128
C_HEADS = 8
C_HEAD_DIM = 64
C_WIDTH = C_HEADS * C_HEAD_DIM
C_WIN_ROWS = 8
C_WIN_COLS = 16
N_EXPERTS = 16
EXPERT_HIDDEN = 1024
EC_CAPACITY_FACTOR = 2
N_BRANCHES = 3

A_COLS = 3 * A_WIDTH + 2 * A_DECAY_LORA + 2 * A_ICLR_LORA + A_GATE_LORA
B_COLS = 2 * B_WIDTH
C_COLS = 3 * C_WIDTH
G_COLS = N_BRANCHES * D_MODEL
IN_COLS = A_COLS + B_COLS + C_COLS + G_COLS

DEEPNORM_ALPHA = (2 * DEPTH) ** 0.25
DEEPNORM_BETA = (8 * DEPTH) ** -0.25
LN_EPS = 1e-5
GN_EPS = 64e-5

kernel_name = "hybrid_rwkv7_gmlp_natten_ec_encoder"


def _layer_norm(x, g, b, eps=LN_EPS):
    xf = x.astype(jnp.float32)
    mu = xf.mean(-1, keepdims=True)
    var = jnp.square(xf - mu).mean(-1, keepdims=True)
    return ((xf - mu) * lax.rsqrt(var + eps) * g + b).astype(x.dtype)


def _centred_shift(p, mu_prev, mu_next):
    zero = jnp.zeros_like(p[:, :1])
    prev = jnp.concatenate([zero, p[:, :-1]], axis=1)
    nxt = jnp.concatenate([p[:, 1:], zero], axis=1)
    return p + mu_prev * (prev - p) + mu_next * (nxt - p)


def _wkv7_scan(r, w, k, v, a, b, reverse):
    bsz, _, h, n = r.shape

    def step(S, inp):
        r_t, w_t, k_t, v_t, a_t, b_t = inp
        sa = jnp.einsum('bhij,bhj->bhi', S, a_t)
        S = S * w_t[:, :, None, :] + sa[..., :, None] * b_t[:, :, None, :] + v_t[..., :, None] * k_t[:, :, None, :]
        return S, jnp.einsum('bhij,bhj->bhi', S, r_t)

    xs = tuple(jnp.swapaxes(t, 0, 1) for t in (r, w, k, v, a, b))
    s0 = jnp.zeros((bsz, h, n, n), jnp.float32)
    _, y = lax.scan(step, s0, xs, reverse=reverse)
    return jnp.swapaxes(y, 0, 1)


def _rwkv7_bidir(pa, mu_prev, mu_next, decay_w0, decay_w2, iclr_a0, iclr_a2, gate_g2, k_k, k_a, r_k, gn_g, gn_b):
    bsz, t, _ = pa.shape
    dt = pa.dtype
    f32 = jnp.float32
    pa = _centred_shift(pa, mu_prev, mu_next).astype(f32)
    splits = np.cumsum([A_WIDTH] * 3 + [A_DECAY_LORA] * 2 + [A_ICLR_LORA] * 2).tolist()
    r, k, v, wd_f, wd_b, ad_f, ad_b, gd = jnp.split(pa, splits, axis=-1)
    heads = lambda z: z.reshape(bsz, t, A_HEADS, A_HEAD_DIM)
    kk = heads(k * k_k.astype(f32))
    kk = kk / jnp.maximum(jnp.sqrt(jnp.sum(kk * kk, -1, keepdims=True)), 1e-12)
    rh, vh = heads(r), heads(v)
    y = 0.0
    k_bonus = 0.0
    for d, (wd, ad, rev) in enumerate(((wd_f, ad_f, False), (wd_b, ad_b, True))):
        w_log = -jax.nn.softplus(-(decay_w0[d].astype(f32) + jnp.tanh(wd) @ decay_w2[d].astype(f32))) - 0.5
        decay = jnp.exp(-jnp.exp(w_log))
        a = jax.nn.sigmoid(iclr_a0[d].astype(f32) + ad @ iclr_a2[d].astype(f32))
        k_d = k * (1.0 + (a - 1.0) * k_a.astype(f32))
        y = y + _wkv7_scan(rh, heads(decay), heads(k_d), vh, -kk, kk * heads(a), rev)
        k_bonus = k_bonus + k_d
    mu = y.mean(-1, keepdims=True)
    var = jnp.square(y - mu).mean(-1, keepdims=True)
    y = ((y - mu) * lax.rsqrt(var + GN_EPS)).reshape(bsz, t, A_WIDTH) * gn_g.astype(f32) + gn_b.astype(f32)
    bonus = jnp.sum(rh * heads(k_bonus) * r_k.astype(f32), -1, keepdims=True) * vh
    y = y + bonus.reshape(bsz, t, A_WIDTH)
    g = jax.nn.sigmoid(gd) @ gate_g2.astype(f32)
    return (y * g).astype(dt)


def _chunked_spatial_gating(pb, sg_ln_g, sg_ln_b, sg_w, sg_b):
    bsz, t, _ = pb.shape
    z = jax.nn.gelu(pb)
    u, v = jnp.split(z, 2, axis=-1)
    v = _layer_norm(v, sg_ln_g, sg_ln_b)
    vc = v.reshape(bsz, t // B_CHUNK, B_CHUNK, B_GROUPS, B_GROUP_DIM)
    mixed = jnp.einsum('gij,bcjgd->bcigd', sg_w, vc) + jnp.swapaxes(sg_b, 0, 1)[None, None, :, :, None]
    return u * mixed.reshape(bsz, t, B_WIDTH)


def _neighbourhood_attention(pc, rpb):
    bsz, t, _ = pc.shape
    rows = t // GRID_W
    kr = min(C_WIN_ROWS, rows)
    kc = C_WIN_COLS
    q, k, v = jnp.split(pc, 3, axis=-1)
    grid = lambda z: z.reshape(bsz, rows, GRID_W, C_HEADS, C_HEAD_DIM)
    q, k, v = grid(q), grid(k), grid(v)
    cols = np.arange(GRID_W)
    col_start = np.clip(cols - kc // 2, 0, GRID_W - kc)
    col_idx = col_start[:, None] + np.arange(kc)[None, :]
    col_off = col_idx - cols[:, None] + (C_WIN_COLS - 1)
    scale = C_HEAD_DIM ** -0.5

    def one_row(i):
        r0 = jnp.clip(i - kr // 2, 0, rows - kr)
        q_row = lax.dynamic_index_in_dim(q, i, axis=1, keepdims=False)
        k_win = lax.dynamic_slice_in_dim(k, r0, kr, axis=1)[:, :, col_idx]
        v_win = lax.dynamic_slice_in_dim(v, r0, kr, axis=1)[:, :, col_idx]
        row_off = r0 + jnp.arange(kr) - i + (C_WIN_ROWS - 1)
        bias = jnp.take(rpb, row_off, axis=1)[:, :, col_off]
        s = (jnp.einsum('bjhd,brjchd->bhjrc', q_row, k_win).astype(jnp.float32) * scale
             + jnp.transpose(bias, (0, 2, 1, 3))[None].astype(jnp.float32))
        p = jax.nn.softmax(s.reshape(bsz, C_HEADS, GRID_W, kr * kc), axis=-1).reshape(s.shape)
        return jnp.einsum('bhjrc,brjchd->bjhd', p.astype(pc.dtype), v_win)

    out = lax.map(one_row, jnp.arange(rows))
    return jnp.moveaxis(out, 0, 1).reshape(bsz, t, C_WIDTH)


def _mixer_block(x, w_in, mu_prev, mu_next, decay_w0, decay_w2, iclr_a0, iclr_a2, gate_g2, k_k, k_a, r_k,
                 gn_g, gn_b, sg_ln_g, sg_ln_b, sg_w, sg_b, rpb, p_a, p_b, p_c, w_out):
    proj = x @ w_in
    pa, pb, pc, pg = jnp.split(proj, [A_COLS, A_COLS + B_COLS, A_COLS + B_COLS + C_COLS], axis=-1)
    ya = _rwkv7_bidir(pa, mu_prev, mu_next, decay_w0, decay_w2, iclr_a0, iclr_a2, gate_g2, k_k, k_a, r_k,
                      gn_g, gn_b) @ p_a
    yb = _chunked_spatial_gating(pb, sg_ln_g, sg_ln_b, sg_w, sg_b) @ p_b
    yc = _neighbourhood_attention(pc, rpb) @ p_c
    ga, gb, gc = jnp.split(jax.nn.sigmoid(pg), N_BRANCHES, axis=-1)
    return (ga * ya + gb * yb + gc * yc) @ w_out


def _expert_choice_moe(x, w_router, e_gate, e_up, e_down):
    bsz, t, d = x.shape
    n = bsz * t
    cap = EC_CAPACITY_FACTOR * n // N_EXPERTS
    xt = x.reshape(n, d)
    aff = jax.nn.softmax((xt @ w_router).astype(jnp.float32), axis=-1)
    gate, idx = lax.top_k(aff.T, cap)
    xe = jnp.take(xt, idx, axis=0)
    h = jax.nn.silu(jnp.einsum('ecd,edf->ecf', xe, e_gate)) * jnp.einsum('ecd,edf->ecf', xe, e_up)
    ye = jnp.einsum('ecf,efd->ecd', h, e_down) * gate[..., None].astype(x.dtype)
    y = jnp.zeros_like(xt).at[idx.reshape(-1)].add(ye.reshape(-1, d))
    return y.reshape(bsz, t, d)


def _trunk(x, w_in, mu_prev, mu_next, decay_w0, decay_w2, iclr_a0, iclr_a2, gate_g2, k_k, k_a, r_k,
           gn_g, gn_b, sg_ln_g, sg_ln_b, sg_w, sg_b, rpb, p_a, p_b, p_c, w_out, ln_mix_g, ln_mix_b,
           w_router, e_gate, e_up, e_down, ln_ffn_g, ln_ffn_b):
    for l in range(DEPTH):
        h = _mixer_block(x, w_in[l], mu_prev[l], mu_next[l], decay_w0[l], decay_w2[l], iclr_a0[l], iclr_a2[l],
                         gate_g2[l], k_k[l], k_a[l], r_k[l], gn_g[l], gn_b[l], sg_ln_g[l], sg_ln_b[l], sg_w[l],
                         sg_b[l], rpb[l], p_a[l], p_b[l], p_c[l], w_out[l])
        x = _layer_norm(DEEPNORM_ALPHA * x + h, ln_mix_g[l], ln_mix_b[l])
        h = _expert_choice_moe(x, w_router[l], e_gate[l], e_up[l], e_down[l])
        x = _layer_norm(DEEPNORM_ALPHA * x + h, ln_ffn_g[l], ln_ffn_b[l])
    return x


def setup_inputs(seed: int = 0) -> dict:
    key = jax.random.key(seed)
    ks = iter(jax.random.split(key, 48))
    nrm = lambda shape, scale: jax.random.normal(next(ks), shape, jnp.float32) * scale
    uni = lambda shape, lo, hi: jax.random.uniform(next(ks), shape, jnp.float32, lo, hi)
    L, D = DEPTH, D_MODEL
    return {
        "x_prompt": nrm((BATCH, SEQ, D), 1.0),
        "x_sample": nrm((DEC_BATCH, DEC_SEQ, D), 1.0),
        "w_in": nrm((L, D, IN_COLS), D ** -0.5),
        "mu_prev": uni((L, A_COLS), 0.0, 0.5),
        "mu_next": uni((L, A_COLS), 0.0, 0.5),
        "decay_w0": uni((L, 2, A_WIDTH), -6.0, -1.0),
        "decay_w2": nrm((L, 2, A_DECAY_LORA, A_WIDTH), 0.1 * A_DECAY_LORA ** -0.5),
        "iclr_a0": nrm((L, 2, A_WIDTH), 0.1),
        "iclr_a2": nrm((L, 2, A_ICLR_LORA, A_WIDTH), 0.5 * A_ICLR_LORA ** -0.5),
        "gate_g2": nrm((L, A_GATE_LORA, A_WIDTH), A_GATE_LORA ** -0.5),
        "k_k": 0.85 + nrm((L, A_WIDTH), 0.02),
        "k_a": 1.0 + nrm((L, A_WIDTH), 0.02),
        "r_k": nrm((L, A_HEADS, A_HEAD_DIM), 0.1),
        "gn_g": 1.0 + nrm((L, A_WIDTH), 0.02),
        "gn_b": nrm((L, A_WIDTH), 0.02),
        "sg_ln_g": 1.0 + nrm((L, B_WIDTH), 0.02),
        "sg_ln_b": nrm((L, B_WIDTH), 0.02),
        "sg_w": nrm((L, B_GROUPS, B_CHUNK, B_CHUNK), B_CHUNK ** -0.5),
        "sg_b": 1.0 + nrm((L, B_GROUPS, B_CHUNK), 0.02),
        "rpb": nrm((L, C_HEADS, 2 * C_WIN_ROWS - 1, 2 * C_WIN_COLS - 1), 0.1),
        "p_a": nrm((L, A_WIDTH, D), A_WIDTH ** -0.5),
        "p_b": nrm((L, B_WIDTH, D), B_WIDTH ** -0.5),
        "p_c": nrm((L, C_WIDTH, D), C_WIDTH ** -0.5),
        "w_out": nrm((L, D, D), D ** -0.5 * DEEPNORM_BETA),
        "ln_mix_g": 1.0 + nrm((L, D), 0.02),
        "ln_mix_b": nrm((L, D), 0.02),
        "w_router": nrm((L, D, N_EXPERTS), D ** -0.5),
        "e_gate": nrm((L, N_EXPERTS, D, EXPERT_HIDDEN), D ** -0.5),
        "e_up": nrm((L, N_EXPERTS, D, EXPERT_HIDDEN), D ** -0.5),
        "e_down": nrm((L, N_EXPERTS, EXPERT_HIDDEN, D), EXPERT_HIDDEN ** -0.5 * DEEPNORM_BETA),
        "ln_ffn_g": 1.0 + nrm((L, D), 0.02),
        "ln_ffn_b": nrm((L, D), 0.02),
    }


def reference(x_prompt, x_sample, w_in, mu_prev, mu_next, decay_w0, decay_w2, iclr_a0, iclr_a2, gate_g2,
              k_k, k_a, r_k, gn_g, gn_b, sg_ln_g, sg_ln_b, sg_w, sg_b, rpb, p_a, p_b, p_c, w_out,
              ln_mix_g, ln_mix_b, w_router, e_gate, e_up, e_down, ln_ffn_g, ln_ffn_b):
    weights = (w_in, mu_prev, mu_next, decay_w0, decay_w2, iclr_a0, iclr_a2, gate_g2, k_k, k_a, r_k,
               gn_g, gn_b, sg_ln_g, sg_ln_b, sg_w, sg_b, rpb, p_a, p_b, p_c, w_out, ln_mix_g, ln_mix_b,
               w_router, e_gate, e_up, e_down, ln_ffn_g, ln_ffn_b)
    y_prompt = _trunk(x_prompt, *weights)
    y_sample = _trunk(x_sample, *weights)
    return (y_prompt, y_sample)
```

```python
import numpy as np
import concourse.bass as bass
import concourse.mybir as mybir
from concourse.bass_utils import run_bass_kernel_spmd
from contextlib import ExitStack

F32 = mybir.dt.float32
BF16 = mybir.dt.bfloat16
AF = mybir.ActivationFunctionType
ALU = mybir.AluOpType

D_MODEL = 2048
DEPTH = 4
A_WIDTH = 1024
A_COLS = 3488
IN_COLS = 12192
GN_EPS = 64e-5
LN_EPS = 1e-5
ALPHA = (2 * DEPTH) ** 0.25
C0 = -float(np.exp(-0.5))
NCORES = 8
NSLOT = 3

ENGS = ("pe", "dve", "act", "pool", "sp")
NDMA_SEM = 24


class Buf:
    __slots__ = ("name", "w", "r")

    def __init__(self, name=""):
        self.name = name
        self.w = None
        self.r = []


class Op:
    __slots__ = ("eng", "fn", "deps", "sig", "cnt", "dma", "dsem", "dval", "emb")

    def __init__(self, eng, fn, dma):
        self.eng = eng
        self.fn = fn
        self.deps = []
        self.sig = False
        self.cnt = 0
        self.dma = dma
        self.dsem = -1
        self.dval = 0
        self.emb = False


class Prog:
    def __init__(self, nc):
        self.nc = nc
        self.ops = {e: [] for e in ENGS}
        self.nops = 0

    def add(self, eng, fn, reads=(), writes=(), dma=False):
        op = Op(eng, fn, dma)
        self.nops += 1
        deps = set()
        for b in reads:
            if b.w is not None:
                deps.add(b.w)
        for b in writes:
            if b.w is not None:
                deps.add(b.w)
            for r in b.r:
                deps.add(r)
        for d in deps:
            if d.eng == "pe" and eng == "pe" and not d.dma:
                continue
            op.deps.append(d)
            d.sig = True
        for b in reads:
            b.r.append(op)
        for b in writes:
            b.w = op
            b.r = []
        self.ops[eng].append(op)
        return op

    def emit(self):
        nc = self.nc
        with ExitStack() as st:
            csem = {e: st.enter_context(nc.semaphore("c_" + e)) for e in ENGS}
            dsem = {e: [st.enter_context(nc.semaphore("d_%s_%d" % (e, i))) for i in range(NDMA_SEM)]
                    for e in ("sp", "act", "pool")}
            for e in ENGS:
                c = 0
                k = 0
                dcount = [0] * NDMA_SEM
                for op in self.ops[e]:
                    if op.dma:
                        op.dsem = k % NDMA_SEM
                        dcount[op.dsem] += 16
                        op.dval = dcount[op.dsem]
                        k += 1
                    elif op.sig:
                        c += 1
                        op.cnt = c
            final_waits = []
            for e in ("sp", "act", "pool"):
                last = {}
                for op in self.ops[e]:
                    if op.dma:
                        last[op.dsem] = op.dval
                for k, v in last.items():
                    final_waits.append((dsem[e][k], v))
            blk = st.enter_context(nc.Block())

            def run(e):
                def body(eng):
                    seen = {}
                    prev_on_sem = {}
                    for op in self.ops[e]:
                        waits = {}
                        for d in op.deps:
                            if d.dma:
                                key = ("d", d.eng, d.dsem)
                                s, v = dsem[d.eng][d.dsem], d.dval
                            else:
                                key = ("c", d.eng)
                                s, v = csem[d.eng], d.cnt
                            if seen.get(key, 0) >= v:
                                continue
                            if key not in waits or waits[key][1] < v:
                                waits[key] = (s, v)
                        if op.dma:
                            pv = prev_on_sem.get(op.dsem, 0)
                            key = ("d", e, op.dsem)
                            if pv and seen.get(key, 0) < pv and (key not in waits or waits[key][1] < pv):
                                waits[key] = (dsem[e][op.dsem], pv)
                            prev_on_sem[op.dsem] = op.dval
                        wl = list(waits.items())
                        last_wait = None
                        if op.emb and wl:
                            last_wait = wl.pop()
                        for key, (s, v) in wl:
                            eng.wait_ge(s, v)
                            seen[key] = v
                        ins = op.fn(eng)
                        if last_wait is not None:
                            key, (s, v) = last_wait
                            ins._wait_ge(s, v)
                            seen[key] = v
                        if op.dma:
                            ins.then_inc(dsem[e][op.dsem], 16)
                        elif op.sig:
                            ins.then_inc(csem[e], 1)
                    if e == "sp":
                        for s, v in final_waits:
                            eng.wait_ge(s, v)
                return body

            blk.tensor(run("pe"))
            blk.vector(run("dve"))
            blk.scalar(run("act"))
            blk.gpsimd(run("pool"))
            blk.sync(run("sp"))

    def tt(P, eng, out, in0, in1, op, r=(), w=()):
        o = P.add(eng, lambda e: e.tensor_tensor(out=out, in0=in0, in1=in1, op=op), reads=r, writes=w)
        o.emb = True
        return o

    def ts(P, eng, out, in0, s1, s2, op0, op1=None, r=(), w=()):
        if op1 is None:
            o = P.add(eng, lambda e: e.tensor_scalar(out=out, in0=in0, scalar1=s1, scalar2=None, op0=op0), reads=r, writes=w)
        else:
            o = P.add(eng, lambda e: e.tensor_scalar(out=out, in0=in0, scalar1=s1, scalar2=s2, op0=op0, op1=op1), reads=r, writes=w)
        o.emb = True
        return o

    def stt(P, out, in0, scalar, in1, op0, op1, r=(), w=()):
        o = P.add("dve", lambda e: e.scalar_tensor_tensor(out=out, in0=in0, scalar=scalar, in1=in1, op0=op0, op1=op1), reads=r, writes=w)
        o.emb = True
        return o

    def act(P, out, in_, func, r=(), w=(), scale=1.0, bias=None):
        if bias is not None:
            o = P.add("act", lambda e: e.activation(out=out, in_=in_, func=func, scale=scale, bias=bias), reads=r, writes=w)
        else:
            o = P.add("act", lambda e: e.activation(out=out, in_=in_, func=func, scale=scale), reads=r, writes=w)
        o.emb = True
        return o

    def cp(P, eng, out, in_, r=(), w=()):
        if eng == "act":
            o = P.add(eng, lambda e: e.copy(out=out, in_=in_), reads=r, writes=w)
        else:
            o = P.add(eng, lambda e: e.tensor_copy(out=out, in_=in_), reads=r, writes=w)
        o.emb = True
        return o

    def mm(P, out, lhsT, rhs, start=True, stop=True, r=(), w=()):
        return P.add("pe", lambda e: e.matmul(out, lhsT=lhsT, rhs=rhs, start=start, stop=stop), reads=r, writes=w)

    def tr(P, out, in_, ident, r=(), w=()):
        return P.add("pe", lambda e: e.transpose(out, in_, ident), reads=r, writes=w)

    def dma(P, eng, out, in_, r=(), w=()):
        return P.add(eng, lambda e: e.dma_start(out=out, in_=in_), reads=r, writes=w, dma=True)

    def memset(P, eng, out, val, w=()):
        return P.add(eng, lambda e: e.memset(out, val), writes=w)

    def recip(P, out, in_, r=(), w=()):
        return P.add("dve", lambda e: e.reciprocal(out=out, in_=in_), reads=r, writes=w)


class Pool:
    def __init__(self, st, nc, name, shape, dtype, n, psum=False):
        self.t = []
        for i in range(n):
            if psum:
                t = st.enter_context(nc.psum_tensor("%s%d" % (name, i), shape, dtype))
            else:
                t = st.enter_context(nc.sbuf_tensor("%s%d" % (name, i), shape, dtype))
            self.t.append((t, Buf("%s%d" % (name, i))))
        self.i = 0

    def get(self):
        r = self.t[self.i % len(self.t)]
        self.i += 1
        return r


def host_consts():
    s = np.arange(128)[:, None]
    t = np.arange(128)[None, :]
    msk = np.stack([(s <= t), (s < t), (s >= t), (s > t)], 1).astype(np.float32)
    bo = np.zeros((128, 128), np.float32)
    bo[:64, :64] = 1
    bo[64:, 64:] = 1
    cm = np.concatenate([msk.reshape(128, 512), np.eye(128, dtype=np.float32), bo], 1)
    return np.ascontiguousarray(cm)


def scan_pair(P, nc, K, T, hp, r, k, v, y_out, bpre, bufs, ps):
    NCH = T // 128
    br, bk, bv, by, bbp = bufs
    bc = K["bc"]
    tri, idf, idb, bones, msk = K["tri"], K["idf"], K["idb"], K["bones"], K["msk"]
    thw, bthw, adb, badb = K["thw"], K["bthw"], K["adb"], K["badb"]
    w2b, a2b, w0r, ones1 = K["w2b"], K["a2b"], K["w0r"], K["ones1"]
    pc = K["pc"]
    hcol = slice(hp * 128, (hp + 1) * 128)
    kk_c, ka_c, ka1_c, rk_c = pc[:, hp, 0:1], pc[:, hp, 1:2], pc[:, hp, 2:3], pc[:, hp, 3:4]
    S = K["S"]
    cpool = K["cpool"]
    vT, bvT = S["vT"]
    AR, bAR = S["AR"]
    KT, bKT = S["KT"]
    BT, bBT = S["BT"]
    KH, bKH = S["KH"]
    BH, bBH = S["BH"]
    WC, bWC = S["WC"]
    TT, bTT = S["TT"]
    ARB, bARB = S["ARB"]
    ARK, bARK = S["ARK"]
    ZT, bZT = S["ZT"]
    SS, bSS = S["SS"]
    SSb, bSSb = S["SSb"]
    XZ, bXZ = S["XZ"]
    UT, bUT = S["UT"]
    bpool = K["bpool"]

    for c in range(NCH):
        pt, bpt = ps.get()
        P.tr(pt[:, 0:128], v[:, c * 128:(c + 1) * 128], idf[:], r=[bv, bc], w=[bpt])
        P.cp("act", vT[:, c, :], pt[:, 0:128], r=[bpt], w=[bvT])

    for d in range(2):
        fwd = (d == 0)
        iT, iTs, iTr = (0, 1, 3) if fwd else (2, 3, 1)
        m_i, m_s = (0, 1) if fwd else (2, 3)
        m_sT = 3 if fwd else 1
        ds_ = slice(d * 64, (d + 1) * 64)
        for c in range(NCH):
            cs = slice(c * 128, (c + 1) * 128)
            kk, bkk = cpool.get()
            sq, bsq = cpool.get()
            kkn, bkkn = cpool.get()
            P.ts("dve", kk[:], k[:, cs], kk_c, None, ALU.mult, r=[bk, bc], w=[bkk])
            P.tt("pool", sq[:], kk[:], kk[:], ALU.mult, r=[bkk], w=[bsq])
            p0, bp0 = ps.get()
            P.mm(p0[:, 0:128], bones[:], sq[:], r=[bsq, bc], w=[bp0])
            p1, bp1 = ps.get()
            P.mm(p1[:, 0:128], a2b[ds_, hcol], adb[ds_, cs], r=[bc, badb], w=[bp1])
            P.mm(p1[:, 128:256], w2b[ds_, hcol], thw[ds_, cs], r=[bc, bthw], w=[bp1])
            P.act(kkn[:], p0[:, 0:128], AF.Sqrt, r=[bp0], w=[bkkn])
            P.ts("dve", kkn[:], kkn[:], 1e-12, None, ALU.max, r=[bkkn], w=[bkkn])
            P.recip(kkn[:], kkn[:], r=[bkkn], w=[bkkn])
            P.tt("dve", kkn[:], kkn[:], kk[:], ALU.mult, r=[bkkn, bkk], w=[bkkn])
            al, bal = cpool.get()
            P.act(al[:], p1[:, 0:128], AF.Sigmoid, r=[bp1, bc], w=[bal], bias=pc[:, hp, 4 + d:5 + d])
            sgF, bsgF = cpool.get()
            P.act(sgF[:], p1[:, 128:256], AF.Sigmoid, r=[bp1, bc], w=[bsgF], bias=pc[:, hp, 8 + d:9 + d])
            p2, bp2 = ps.get()
            P.tr(p2[:, 0:128], sgF[:], idf, r=[bsgF, bc], w=[bp2])
            sg, bsg = cpool.get()
            P.cp("act", sg[:], p2[:, 0:128], r=[bp2], w=[bsg])
            kd, bkd = cpool.get()
            bb, bbb = cpool.get()
            P.ts("dve", kd[:], al[:], ka_c, ka1_c, ALU.mult, ALU.add, r=[bal, bc], w=[bkd])
            P.tt("dve", kd[:], kd[:], k[:, cs], ALU.mult, r=[bkd, bk], w=[bkd])
            P.tt("pool", bb[:], kkn[:], al[:], ALU.mult, r=[bkkn, bal], w=[bbb])
            if d == 0:
                P.stt(bpre[:, cs], kd[:], rk_c, r[:, cs], ALU.mult, ALU.mult, r=[bkd, br, bc], w=[bbp])
            else:
                t2, bt2 = cpool.get()
                P.stt(t2[:], kd[:], rk_c, r[:, cs], ALU.mult, ALU.mult, r=[bkd, br, bc], w=[bt2])
                P.tt("pool", bpre[:, cs], bpre[:, cs], t2[:], ALU.add, r=[bt2, bbp], w=[bbp])
            pt, bpt = ps.get()
            P.mm(pt[:, 0:128], sg[:], tri[:, iT, :], r=[bsg, bc], w=[bpt])
            P.mm(pt[:, 128:256], sg[:], tri[:, iTs, :], r=[bsg, bc], w=[bpt])
            P.mm(pt[:, 256:384], tri[:, iTr, :], sg[:], r=[bsg, bc], w=[bpt])
            Ep, bEp = cpool.get()
            Em, bEm = cpool.get()
            Ea, bEa = cpool.get()
            Eh, bEh = cpool.get()
            P.act(Ep[:], pt[:, 0:128], AF.Exp, r=[bpt], w=[bEp])
            P.act(Em[:], pt[:, 0:128], AF.Exp, r=[bpt], w=[bEm], scale=-1.0)
            P.act(Ea[:], pt[:, 128:256], AF.Exp, r=[bpt], w=[bEa])
            P.act(Eh[:], pt[:, 256:384], AF.Exp, r=[bpt], w=[bEh])
            last = 127 if fwd else 0
            P.cp("act", WC[:, c:c + 1], Ep[:, last:last + 1], r=[bEp], w=[bWC])
            P.tt("dve", AR[:, c, 1, :], r[:, cs], Ep[:], ALU.mult, r=[br, bEp], w=[bAR])
            P.stt(AR[:, c, 0, :], kkn[:], -1.0, Ea[:], ALU.mult, ALU.mult, r=[bkkn, bEa], w=[bAR])
            P.tt("pool", KT[:, c, :], kd[:], Em[:], ALU.mult, r=[bkd, bEm], w=[bKT])
            P.tt("pool", BT[:, c, :], bb[:], Em[:], ALU.mult, r=[bbb, bEm], w=[bBT])
            pt2, bpt2 = ps.get()
            P.tr(pt2[:, 0:128], kd[:], idf[:], r=[bkd, bc], w=[bpt2])
            P.tr(pt2[:, 128:256], bb[:], idf[:], r=[bbb, bc], w=[bpt2])
            P.tt("dve", KH[:, c, :], pt2[:, 0:128], Eh[:], ALU.mult, r=[bpt2, bEh], w=[bKH])
            P.tt("dve", BH[:, c, :], pt2[:, 128:256], Eh[:], ALU.mult, r=[bpt2, bEh], w=[bBH])

        G = 4
        for h in range(2):
            hs = slice(h * 64, (h + 1) * 64)
            for c0 in range(0, NCH, G):
                g = min(G, NCH - c0)
                W = g * 128
                pp = [ps.get() for _ in range(5)]
                for i in range(g):
                    c = c0 + i
                    o = slice(i * 128, (i + 1) * 128)
                    P.mm(pp[0][0][:, o], BT[hs, c, :], AR[hs, c, 0, :], r=[bBT, bAR], w=[pp[0][1]])
                    P.mm(pp[1][0][:, o], BT[hs, c, :], AR[hs, c, 1, :], r=[bBT, bAR], w=[pp[1][1]])
                    P.mm(pp[2][0][:, o], KT[hs, c, :], AR[hs, c, 0, :], r=[bKT, bAR], w=[pp[2][1]])
                    P.mm(pp[3][0][:, o], KT[hs, c, :], AR[hs, c, 1, :], r=[bKT, bAR], w=[pp[3][1]])
                    P.mm(pp[4][0][:, o], AR[hs, c, 0, :], BT[hs, c, :], r=[bBT, bAR], w=[pp[4][1]])
                NN, bNN = bpool["N"].get()
                NT_, bNT = bpool["NT"].get()
                AAK, bAAK = bpool["AAK"].get()
                Tc, bTc = bpool["T"].get()
                for i in range(g):
                    o = slice(i * 128, (i + 1) * 128)
                    P.tt("dve", NN[:, i, :], pp[0][0][:, o], msk[:, m_s, :], ALU.mult, r=[pp[0][1], bc], w=[bNN])
                    P.tt("dve", ARB[:, h, c0 + i, :], pp[1][0][:, o], msk[:, m_i, :], ALU.mult, r=[pp[1][1], bc], w=[bARB])
                    P.tt("dve", AAK[:, i, :], pp[2][0][:, o], msk[:, m_s, :], ALU.mult, r=[pp[2][1], bc], w=[bAAK])
                    P.tt("dve", ARK[:, h, c0 + i, :], pp[3][0][:, o], msk[:, m_i, :], ALU.mult, r=[pp[3][1], bc], w=[bARK])
                    P.tt("dve", NT_[:, i, :], pp[4][0][:, o], msk[:, m_sT, :], ALU.mult, r=[pp[4][1], bc], w=[bNT])
                    P.tt("pool", Tc[:, i, :], NN[:, i, :], idb[:], ALU.add, r=[bNN, bc], w=[bTc])
                Nc, bNc, NTc, bNTc = NN, bNN, NT_, bNT
                for kk_ in range(1, 7):
                    q1, bq1 = ps.get()
                    q2, bq2 = ps.get()
                    q3, bq3 = ps.get()
                    Nn, bNn = bpool["N"].get()
                    NTn, bNTn = bpool["NT"].get()
                    for i in range(g):
                        o = slice(i * 128, (i + 1) * 128)
                        if kk_ < 6:
                            P.mm(q1[:, o], NTc[:, i, :], Nc[:, i, :], r=[bNc, bNTc], w=[bq1])
                        P.mm(q2[:, o], Nc[:, i, :], NTc[:, i, :], r=[bNc, bNTc], w=[bq2])
                    if kk_ < 6:
                        P.cp("act", Nn[:, 0:g, :], q1[:, 0:W].rearrange("p (g t) -> p g t", g=g), r=[bq1], w=[bNn])
                    P.cp("dve", NTn[:, 0:g, :], q2[:, 0:W].rearrange("p (g t) -> p g t", g=g), r=[bq2], w=[bNTn])
                    for i in range(g):
                        o = slice(i * 128, (i + 1) * 128)
                        P.mm(q3[:, o], NTn[:, i, :], Tc[:, i, :], r=[bNTn, bTc], w=[bq3])
                    if kk_ < 6:
                        Tn, bTn = bpool["T"].get()
                        P.tt("dve", Tn[:, 0:g, :], q3[:, 0:W].rearrange("p (g t) -> p g t", g=g), Tc[:, 0:g, :], ALU.add, r=[bq3, bTc], w=[bTn])
                        Tc, bTc = Tn, bTn
                    else:
                        P.tt("dve", TT[:, h, c0:c0 + g, :], q3[:, 0:W].rearrange("p (g t) -> p g t", g=g), Tc[:, 0:g, :], ALU.add, r=[bq3, bTc], w=[bTT])
                    Nc, bNc, NTc, bNTc = Nn, bNn, NTn, bNTn
                pz, bpz = ps.get()
                for i in range(g):
                    c = c0 + i
                    P.mm(pz[:, i * 64:(i + 1) * 64], AAK[:, i, :], vT[:, c, hs], r=[bAAK, bvT], w=[bpz])
                P.cp("act", ZT[:, h, c0:c0 + g, :], pz[:, 0:g * 64].rearrange("p (g t) -> p g t", g=g), r=[bpz], w=[bZT])

        P.memset("pool", SS[:], 0.0, w=[bSS])
        P.memset("pool", SSb[:], 0.0, w=[bSSb])
        order = range(NCH) if fwd else range(NCH - 1, -1, -1)
        for c in order:
            cs = slice(c * 128, (c + 1) * 128)
            px, bpx = ps.get()
            P.mm(px[:, 0:128], AR[:, c, 0, :], SSb[:], r=[bAR, bSSb], w=[bpx])
            P.tt("dve", XZ[:], px[:, 0:128].rearrange("p (h i) -> p h i", h=2), ZT[:, :, c, :], ALU.add, r=[bpx, bZT], w=[bXZ])
            pu, bpu = ps.get()
            for h in range(2):
                P.mm(pu[:, h * 64:(h + 1) * 64], TT[:, h, c, :], XZ[:, h, :], r=[bTT, bXZ], w=[bpu])
            P.cp("act", UT[:], pu[:, 0:128].rearrange("p (h i) -> p h i", h=2), r=[bpu], w=[bUT])
            py, bpy = ps.get()
            pS, bpS = ps.get()
            P.mm(py[:, 0:128], SSb[:], AR[:, c, 1, :], start=True, stop=False, r=[bSSb, bAR], w=[bpy])
            for h in range(2):
                hs = slice(h * 64, (h + 1) * 64)
                P.mm(py[hs, 0:128], UT[:, h, :], ARB[:, h, c, :], start=False, stop=False, r=[bUT, bARB], w=[bpy])
                P.mm(py[hs, 0:128], vT[:, c, hs], ARK[:, h, c, :], start=False, stop=(h == 1), r=[bvT, bARK], w=[bpy])
            for h in range(2):
                hs = slice(h * 64, (h + 1) * 64)
                P.mm(pS[hs, hs], BH[:, c, hs], UT[:, h, :], start=True, stop=False, r=[bBH, bUT], w=[bpS])
                P.mm(pS[hs, hs], KH[:, c, hs], vT[:, c, hs], start=False, stop=True, r=[bKH, bvT], w=[bpS])
            if d == 0:
                P.cp("act", y_out[:, cs], py[:, 0:128], r=[bpy], w=[by])
            else:
                P.tt("dve", y_out[:, cs], py[:, 0:128], y_out[:, cs], ALU.add, r=[bpy, by], w=[by])
            for h in range(2):
                hs = slice(h * 64, (h + 1) * 64)
                P.stt(SS[hs, hs], SS[hs, hs], WC[hs, c:c + 1], pS[hs, hs], ALU.mult, ALU.add, r=[bSS, bWC, bpS], w=[bSS])
            P.cp("act", SSb[:], SS[:], r=[bSS], w=[bSSb])


B0 = A_COLS
CQ = B0 + 1024
CK = CQ + 512
CV = CK + 512
G0 = CV + 512
NEXP = 16
EH = 1024
NBIS = 36


def build_program(L=DEPTH, NS=NSLOT, T=2048, groups=None, debug=None):
    nc = bass.Bass("TRN2", target_bir_lowering=False)
    NCH = T // 128
    NT = NS * T
    NR = T // 64
    if groups is None:
        groups = [[(c, 0) for c in range(4)], [(c, s) for c in range(NCORES) for s in (1, 2)]]
    NG = len(groups)
    slot_group = {}
    for gi, g in enumerate(groups):
        for (c, s_) in g:
            slot_group.setdefault(s_, gi)

    def din(name, shape, dt=F32):
        return nc.dram_tensor(name, shape, dt, kind="ExternalInput").ap()

    def dscr(name, shape, dt=F32, dump=False):
        if dump and debug:
            return nc.dram_tensor(name, shape, dt, kind="ExternalOutput").ap()
        return nc.dram_tensor(name, shape, dt).ap()

    x_in = din("x", [NT, D_MODEL])
    w_in_sh = din("w_in", [L, D_MODEL // NCORES, IN_COLS])
    cmat = din("cmat", [128, 768])
    mu_d = din("mu", [L, 128, 2, 28])
    pcs_d = din("pcs", [L, 128, 8, 10])
    w2_d = din("w2", [L, 128, A_WIDTH])
    a2_d = din("a2", [L, 128, A_WIDTH])
    g2_d = din("g2", [L, 160, A_WIDTH])
    sgln_d = din("sgln", [L, 2, 512])
    sgwT_d = din("sgwT", [L, 128, 8, 128])
    sgb_d = din("sgb", [L, 128, 8])
    rpbT_d = din("rpbT", [L, 8, 64, 15, 64])
    nmask_d = din("nmask", [64, 64])
    pa_sh = din("p_a", [L, 1024 // NCORES, D_MODEL])
    pb_sh = din("p_b", [L, 512 // NCORES, D_MODEL])
    pc_sh = din("p_c", [L, 512 // NCORES, D_MODEL])
    wout_sh = din("w_out", [L, D_MODEL // NCORES, D_MODEL])
    ln1_d = din("ln1", [L, 2, D_MODEL])
    ln2_d = din("ln2", [L, 2, D_MODEL])
    wr_d = din("w_router", [L, D_MODEL, NEXP])
    eg_d = din("eg", [L, 2, D_MODEL, EH])
    eu_d = din("eu", [L, 2, D_MODEL, EH])
    ed_d = din("ed", [L, 2, EH, D_MODEL])
    y_out = nc.dram_tensor("y", [NT, D_MODEL], F32, kind="ExternalOutput").ap()
    dbg = {}
    if debug:
        for nm, shp in debug.items():
            dbg[nm] = nc.dram_tensor("dbg_" + nm, shp, F32, kind="ExternalOutput").ap()

    xbuf = [dscr("xbuf0", [NT, D_MODEL]), dscr("xbuf1", [NT, D_MODEL])]
    wsh = [(w_in_sh, D_MODEL // NCORES, IN_COLS), (pa_sh, 1024 // NCORES, D_MODEL), (pb_sh, 512 // NCORES, D_MODEL),
           (pc_sh, 512 // NCORES, D_MODEL), (wout_sh, D_MODEL // NCORES, D_MODEL)]
    wloc = [dscr("wloc%d" % i, [r_, c_]) for i, (_, r_, c_) in enumerate(wsh)]
    wfull = [dscr("wfull%d" % i, [r_ * NCORES, c_]) for i, (_, r_, c_) in enumerate(wsh)]
    w_inF, pa_F, pb_F, pc_F, wout_F = wfull
    rkv = dscr("rkv_scr", [NS, 3072, T])
    gT = dscr("gT_scr", [NS, 6144, T], BF16)
    ybrT = dscr("ybr_scr", [NS, 2048, T], BF16, dump=True)
    uscr = dscr("u_scr", [T, 512], BF16)
    qkT = dscr("qk_scr", [1024, T], BF16, dump=True)
    vrow = dscr("vrow_scr", [64, NR, 8, 65], BF16)
    mT = dscr("mT_scr", [2048, T], BF16, dump=True)
    hres = dscr("hres_scr", [T, D_MODEL], dump=True)
    x1 = dscr("x1_scr", [NT, D_MODEL], dump=True)
    aff = dscr("aff_scr", [NT, NEXP])
    aff_all = dscr("affall_scr", [NCORES * NT, NEXP])
    eloc = [dscr("eloc%d" % i, [2 * D_MODEL * EH], BF16) for i in range(3)]
    eall = [dscr("eall%d" % i, [NEXP * D_MODEL * EH], BF16) for i in range(3)]

    P = Prog(nc)
    with ExitStack() as st:
        def sb(name, shape, dt=F32):
            return st.enter_context(nc.sbuf_tensor("t_" + name, shape, dt)), Buf(name)

        ps = Pool(st, nc, "ps", [128, 512], F32, 7, psum=True)
        psx = Pool(st, nc, "psx", [128, 512], F32, 1, psum=True)
        bc = Buf("consts")
        cm, _ = sb("cm", [128, 768])
        tri, _ = sb("tri", [128, 4, 128])
        idb, _ = sb("idb", [128, 128], BF16)
        onesf, _ = sb("onesf", [128, 128])
        P.dma("sp", cm[:], cmat, w=[bc])
        msk = cm[:, 0:512].rearrange("p (a t) -> p a t", a=4)
        idf = cm[:, 512:640]
        bones = cm[:, 640:768]
        P.ts("dve", tri[:], msk, C0, None, ALU.mult, r=[bc], w=[bc])
        P.cp("dve", idb[:], idf, r=[bc], w=[bc])
        P.memset("pool", onesf[:], 1.0, w=[bc])

        arena, _ = sb("arena", [128, 16 * T], BF16)
        xT, bxT = arena[:, :].rearrange("p (k t) -> p k t", k=16), Buf("xT")
        A2N = max(5 * T + 8, 10248)
        arena2, _ = sb("arena2", [128, A2N])
        ba2 = Buf("arena2")
        rr, brr = arena2[:, 0:T], Buf("rr")
        kk_, bkk_ = arena2[:, T:2 * T], Buf("kk")
        vv, bvv = arena2[:, 2 * T:3 * T], Buf("vv")
        yy, byy = arena2[:, 3 * T:4 * T], Buf("yy")
        raw, braw = arena2[:, 4 * T:5 * T + 2], Buf("raw")
        bpre, bbpre = raw[:, 1:T + 1], braw
        free3 = arena2[:, 0:3 * T]
        freeA = arena2[:, 0:A2N]
        a2bufs = [brr, bkk_, bvv, byy, braw, ba2]
        thw, bthw = sb("thw", [128, T], BF16)
        adb, badb = sb("adb", [128, T], BF16)
        sgd, bsgd = sb("sgd", [128, T], BF16)
        sgd2, bsgd2 = sb("sgd2", [32, T], BF16)
        P.memset("pool", raw[:], 0.0, w=[braw])
        S = {}
        off = 0
        for nm, shp in (("vT", [NCH, 128]), ("AR", [NCH, 2, 128]), ("KT", [NCH, 128]), ("BT", [NCH, 128]), ("KH", [NCH, 128]),
                        ("BH", [NCH, 128]), ("TT", [2, NCH, 128]), ("ARB", [2, NCH, 128]), ("ARK", [2, NCH, 128]), ("ZT", [2, NCH, 64])):
            n = int(np.prod(shp))
            v_ = arena[:, off:off + n]
            if len(shp) == 2:
                v_ = v_.rearrange("p (a b) -> p a b", a=shp[0])
            else:
                v_ = v_.rearrange("p (a b c) -> p a b c", a=shp[0], b=shp[1])
            S[nm] = (v_, Buf("s_" + nm))
            off += n
        assert off <= 16 * T
        for nm, shp, dt in (("WC", [128, NCH], F32), ("SS", [128, 128], F32),
                            ("SSb", [128, 128], BF16), ("XZ", [128, 2, 64], BF16), ("UT", [128, 2, 64], BF16)):
            S[nm] = sb("s_" + nm, shp, dt)
        arena_bufs = [S[n][1] for n in ("vT", "AR", "KT", "BT", "KH", "BH", "TT", "ARB", "ARK", "ZT")]
        fence_t, _ = sb("fence", [128, 8])
        cpool = Pool(st, nc, "cp", [128, 128], F32, 16)
        bpool = {n_: Pool(st, nc, "bp" + n_, [128, 4, 128], BF16, 3) for n_ in ("N", "NT", "T", "AAK")}
        wst = Pool(st, nc, "wst", [128, 16, 128], F32, 1)
        wbf = Pool(st, nc, "wbf", [128, 16, 128], BF16, 4)
        xld = Pool(st, nc, "xld", [128, D_MODEL], F32, 1)
        ost = Pool(st, nc, "ost", [128, 512], F32, 4)
        obf = Pool(st, nc, "obf", [128, 512], BF16, 3)
        mu, _ = sb("mu", [128, 3, 28])
        pcs, _ = sb("pcs", [128, 8, 10])
        wtmp, bwtmp = sb("wtmp", [128, A_WIDTH])
        w2b, _ = sb("w2b", [128, A_WIDTH], BF16)
        a2b, _ = sb("a2b", [128, A_WIDTH], BF16)
        g2b, _ = sb("g2b", [128, A_WIDTH], BF16)
        g2b2, _ = sb("g2b2", [32, A_WIDTH], BF16)
        small, bsmall = sb("small", [128, 64])
        thr, bthr = sb("thr", [128, NG, NEXP])

        brkv = [[Buf("rkv%d_%d" % (s_, i)) for i in range(24)] for s_ in range(NS)]
        bgT = [Buf("gT%d" % s_) for s_ in range(NS)]
        bybr = [[Buf("ybr%d_%d" % (s_, i)) for i in range(16)] for s_ in range(NS)]
        bx = {}

        def xb(t, s_):
            return bx.setdefault((id(t.tensor) if hasattr(t, "tensor") else id(t), s_), Buf("xs"))

        K = dict(bc=bc, tri=tri, idf=idf, idb=idb, bones=bones, msk=msk, thw=thw, bthw=bthw, adb=adb, badb=badb,
                 w2b=w2b, a2b=a2b, w0r=None, ones1=None, pc=pcs, S=S, cpool=cpool, bpool=bpool)

        def fence(bufs):
            P.add("pool", lambda e: e.memset(fence_t[:], 0.0), reads=[], writes=list(bufs))

        def load_wblock(src_ap, n, r=()):
            ws, bws = wst.get()
            wb, bwb = wbf.get()
            P.dma("sp", ws[:, :, 0:n], src_ap.rearrange("(kc p) n -> p kc n", p=128), r=list(r), w=[bws])
            P.cp("pool", wb[:, :, 0:n], ws[:, :, 0:n], r=[bws], w=[bwb])
            return wb, bwb

        proj_order = [(24 * 128, 128), (25 * 128, 128), (26 * 128, 128), (27 * 128, 32)] + [(ti * 128, 128) for ti in range(24)] \
            + [(G0 + gi * 128, 128) for gi in range(48)] + [(CQ + ci * 128, 128) for ci in range(8)]
        nxt_of = {proj_order[i][0]: proj_order[i + 1] for i in range(len(proj_order) - 1)}
        pending = {}

        def proj_fm(l, col0, n, evac):
            if col0 in pending:
                wb, bwb = pending.pop(col0)
            else:
                wb, bwb = load_wblock(w_inF[:, col0:col0 + n], n, r=[bwfull[0]])
            if col0 in nxt_of:
                c1, n1 = nxt_of[col0]
                pending[c1] = load_wblock(w_inF[:, c1:c1 + n1], n1, r=[bwfull[0]])
            for tb in range(0, T, 512):
                w = min(512, T - tb)
                pt, bpt = ps.get()
                for kc in range(16):
                    P.mm(pt[0:n, 0:w], wb[:, kc, 0:n], xT[:, kc, tb:tb + w], start=(kc == 0), stop=(kc == 15), r=[bwb, bxT], w=[bpt])
                evac(pt, bpt, tb, w)

        def shifted(l, ti, n, dst, bdst):
            def ev(pt, bpt, tb, w):
                P.cp("act", raw[0:n, 1 + tb:1 + tb + w], pt[0:n, 0:w], r=[bpt], w=[braw])
            proj_fm(l, ti * 128, n, ev)
            P.ts("dve", dst[0:n, :], raw[0:n, 1:T + 1], mu[0:n, 2, ti:ti + 1], None, ALU.mult, r=[braw, bc], w=[bdst])
            P.stt(dst[0:n, :], raw[0:n, 0:T], mu[0:n, 0, ti:ti + 1], dst[0:n, :], ALU.mult, ALU.add, r=[braw, bdst, bc], w=[bdst])
            P.stt(dst[0:n, :], raw[0:n, 2:T + 2], mu[0:n, 1, ti:ti + 1], dst[0:n, :], ALU.mult, ALU.add, r=[braw, bdst, bc], w=[bdst])

        def build_xT(src, s, bsrc):
            fence([bxT] + arena_bufs)
            for tt_ in range(NCH):
                xl, bxl = xld.get()
                P.dma("sp", xl[:], src[s * T + tt_ * 128: s * T + (tt_ + 1) * 128, :], r=[bsrc], w=[bxl])
                for kq in range(4):
                    pt, bpt = ps.get()
                    for j in range(4):
                        kc = kq * 4 + j
                        P.tr(pt[:, j * 128:(j + 1) * 128], xl[:, kc * 128:(kc + 1) * 128], idf, r=[bxl, bc], w=[bpt])
                    eng = "act" if kq % 2 == 0 else "dve"
                    P.cp(eng, xT[:, kq * 4:(kq + 1) * 4, tt_ * 128:(tt_ + 1) * 128],
                         pt[:, 0:512].rearrange("p (j t) -> p j t", j=4), r=[bpt], w=[bxT])

        def layer_norm_rows(src_tile, bsrc_t, gb, bgb, dst_tile, bdst_t, ncols):
            nchunk = ncols // 512
            st_, bst_ = cpool.get()
            for j in range(nchunk):
                P.add("dve", lambda e, j=j, st_=st_: e.bn_stats(out=st_[:, j * 6:(j + 1) * 6], in_=src_tile[:, j * 512:(j + 1) * 512]),
                      reads=[bsrc_t], writes=[bst_])
            mv, bmv = cpool.get()
            P.add("dve", lambda e: e.bn_aggr(out=mv[:, 0:2], in_=st_[:, 0:6 * nchunk]), reads=[bst_], writes=[bmv])
            P.ts("dve", mv[:, 2:3], mv[:, 1:2], LN_EPS, None, ALU.add, r=[bmv], w=[bmv])
            P.act(mv[:, 2:3], mv[:, 2:3], AF.Sqrt, r=[bmv], w=[bmv])
            P.recip(mv[:, 2:3], mv[:, 2:3], r=[bmv], w=[bmv])
            P.ts("dve", dst_tile[:, 0:ncols], src_tile[:, 0:ncols], mv[:, 0:1], mv[:, 2:3], ALU.subtract, ALU.mult, r=[bsrc_t, bmv], w=[bdst_t])
            P.tt("pool", dst_tile[:, 0:ncols], dst_tile[:, 0:ncols], gb[:, 0, :], ALU.mult, r=[bdst_t, bgb], w=[bdst_t])
            P.tt("pool", dst_tile[:, 0:ncols], dst_tile[:, 0:ncols], gb[:, 1, :], ALU.add, r=[bdst_t, bgb], w=[bdst_t])

        bxin = Buf("xin")
        bxbuf = [Buf("xbuf0"), Buf("xbuf1")]
        bx1 = Buf("x1")
        bhres = Buf("hres")
        baff = Buf("aff")
        baffall = Buf("affall")
        bmT = Buf("mT")
        bus, bqk, bvr = Buf("uscr"), Buf("qk"), Buf("vrow")
        beloc = [Buf("eloc%d" % i) for i in range(3)]
        beall = [Buf("eall%d" % i) for i in range(3)]
        byout = Buf("yout")
        bwloc = [Buf("wloc%d" % i) for i in range(5)]
        bwfull = [Buf("wfull%d" % i) for i in range(5)]

        for l in range(L):
            src, bsrc = (x_in, bxin) if l == 0 else (xbuf[(l - 1) % 2], bxbuf[(l - 1) % 2])
            dst, bdst = (y_out, byout) if l == L - 1 else (xbuf[l % 2], bxbuf[l % 2])
            P.dma("sp", mu[:, 0:2, :], mu_d[l], r=[], w=[bc])
            P.dma("sp", pcs[:], pcs_d[l], w=[bc])
            P.tt("dve", mu[:, 2, :], mu[:, 0, :], mu[:, 1, :], ALU.add, r=[bc], w=[bc])
            P.ts("dve", mu[:, 2, :], mu[:, 2, :], -1.0, 1.0, ALU.mult, ALU.add, r=[bc], w=[bc])
            P.ts("dve", pcs[:, :, 2:3], pcs[:, :, 1:2], -1.0, 1.0, ALU.mult, ALU.add, r=[bc], w=[bc])
            for (srcw, dstb) in ((w2_d, w2b), (a2_d, a2b)):
                P.dma("sp", wtmp[:], srcw[l], r=[], w=[bwtmp])
                P.cp("dve", dstb[:], wtmp[:], r=[bwtmp], w=[bc])
            P.dma("sp", wtmp[:], g2_d[l, 0:128, :], w=[bwtmp])
            P.cp("dve", g2b[:], wtmp[:], r=[bwtmp], w=[bc])
            P.dma("sp", wtmp[0:32, :], g2_d[l, 128:160, :], w=[bwtmp])
            P.cp("dve", g2b2[:], wtmp[0:32, :], r=[bwtmp], w=[bc])
            for wi_, (sh, r_, c_) in enumerate(wsh):
                step = max(1, r_ // 4)
                for r0_ in range(0, r_, step):
                    P.dma("sp", wloc[wi_][r0_:r0_ + step, :], sh[l, r0_:r0_ + step, :], r=[bwfull[wi_]], w=[bwloc[wi_]])
                P.add("pool", lambda e, wi_=wi_: e.collective_compute("AllGather", ALU.bypass, replica_groups=[list(range(NCORES))],
                                                                     ins=[wloc[wi_][:, :]], outs=[wfull[wi_][:, :]]),
                      reads=[bwloc[wi_]], writes=[bwfull[wi_]])
            for wi, (wd_, rows, cols) in enumerate(((eg_d, D_MODEL, EH), (eu_d, D_MODEL, EH), (ed_d, EH, D_MODEL))):
                if wi < 2:
                    oblk = eloc[wi].rearrange("(b p f) -> b p f", p=128, f=2048)
                    for el in range(2):
                        for hb in range(8):
                            wb_, bwb_ = load_wblock(wd_[l, el, :, hb * 128:(hb + 1) * 128], 128)
                            P.dma("sp", oblk[el * 8 + hb], wb_[:, :, :].rearrange("p a b -> p (a b)"), r=[bwb_], w=[beloc[wi]])
                else:
                    flat = wd_[l].rearrange("e r c -> (e r c)").rearrange("(p f) -> p f", p=128)
                    ofl = eloc[wi].rearrange("(p f) -> p f", p=128)
                    F_ = 2 * rows * cols // 128
                    CH = 2048
                    for f0 in range(0, F_, CH):
                        xl, bxl = xld.get()
                        P.dma("sp", xl[:, 0:CH], flat[:, f0:f0 + CH], w=[bxl])
                        ob, bob = wbf.get()
                        obv = ob[:, :, :].rearrange("p a b -> p (a b)")
                        P.cp("pool", obv[:, 0:CH], xl[:, 0:CH], r=[bxl], w=[bob])
                        P.dma("sp", ofl[:, f0:f0 + CH], obv[:, 0:CH], r=[bob], w=[beloc[wi]])
                P.add("pool", lambda e, wi=wi: e.collective_compute("AllGather", ALU.bypass, replica_groups=[list(range(NCORES))],
                                                                   ins=[eloc[wi].rearrange("(a b) -> a b", b=2048)],
                                                                   outs=[eall[wi].rearrange("(a b) -> a b", b=2048)]),
                      reads=[beloc[wi]], writes=[beall[wi]])

            for s in range(NS):
                build_xT(src, s, bsrc)
                fence(a2bufs)
                P.memset("pool", raw[:, 0:1], 0.0, w=[braw])
                P.memset("pool", raw[:, T + 1:T + 2], 0.0, w=[braw])
                shifted(l, 24, 128, yy, byy)
                P.act(thw[:], yy[:], AF.Tanh, r=[byy], w=[bthw])
                shifted(l, 25, 128, yy, byy)
                P.cp("act", adb[:], yy[:], r=[byy], w=[badb])
                shifted(l, 26, 128, yy, byy)
                P.act(sgd[:], yy[:], AF.Sigmoid, r=[byy], w=[bsgd])
                shifted(l, 27, 32, yy, byy)
                P.act(sgd2[:], yy[0:32, :], AF.Sigmoid, r=[byy], w=[bsgd2])
                for ti in range(24):
                    shifted(l, ti, 128, yy, byy)
                    P.dma("sp", rkv[s, ti * 128:(ti + 1) * 128, :], yy[:], r=[byy], w=[brkv[s][ti]])
                for gi in range(48):
                    def evg(pt, bpt, tb, w, gi=gi):
                        ob, bob = obf.get()
                        P.act(ob[:, 0:w], pt[:, 0:w], AF.Sigmoid, r=[bpt], w=[bob])
                        P.dma("sp", gT[s, gi * 128:(gi + 1) * 128, tb:tb + w], ob[:, 0:w], r=[bob], w=[bgT[s]])
                    proj_fm(l, G0 + gi * 128, 128, evg)
                for ci in range(8):
                    def evq(pt, bpt, tb, w, ci=ci):
                        ob, bob = obf.get()
                        P.cp("act", ob[:, 0:w], pt[:, 0:w], r=[bpt], w=[bob])
                        P.dma("sp", qkT[ci * 128:(ci + 1) * 128, tb:tb + w], ob[:, 0:w], r=[bob], w=[bqk])
                    proj_fm(l, CQ + ci * 128, 128, evq)
                wv = [load_wblock(w_inF[:, CV + j * 128:CV + (j + 1) * 128], 128, r=[bwfull[0]]) for j in range(4)]
                for rw in range(NR):
                    pt, bpt = ps.get()
                    for j in range(4):
                        for kc in range(16):
                            P.mm(pt[0:64, j * 128:(j + 1) * 128], xT[:, kc, rw * 64:(rw + 1) * 64], wv[j][0][:, kc, :],
                                 start=(kc == 0), stop=(kc == 15), r=[wv[j][1], bxT], w=[bpt])
                    vt, bvt = ost.get()
                    vtb = vt[0:64, 0:260].bitcast(BF16).rearrange("p (h d) -> p h d", h=8)
                    P.cp("act", vtb[:, :, 0:64], pt[0:64, 0:512].rearrange("p (h d) -> p h d", h=8), r=[bpt], w=[bvt])
                    P.memset("pool", vtb[:, :, 64:65], 1.0, w=[bvt])
                    P.dma("sp", vrow[:, rw, :, :], vtb, r=[bvt], w=[bvr])
                fence(a2bufs)
                lnb = free3[:, 0:1024].rearrange("p (a c) -> p a c", a=2)
                sgw = free3[:, 1024:2048].bitcast(BF16).rearrange("p (g i) -> p g i", g=16)[:, 0:8, :]
                sgwf = free3[:, 2048:3072].rearrange("p (g i) -> p g i", g=8)
                sgbt = small[:, 0:8]
                P.dma("sp", lnb[:, 0, :], sgln_d[l, 0:1, :].to_broadcast([128, 512]), w=[ba2])
                P.dma("sp", lnb[:, 1, :], sgln_d[l, 1:2, :].to_broadcast([128, 512]), w=[ba2])
                P.dma("sp", sgwf, sgwT_d[l], w=[ba2])
                P.cp("dve", sgw, sgwf, r=[ba2], w=[ba2])
                P.dma("sp", sgbt, sgb_d[l], w=[bsmall])
                for cb in range(2):
                    wB = [load_wblock(w_inF[:, B0 + cb * 512 + j * 128:B0 + cb * 512 + (j + 1) * 128], 128, r=[bwfull[0]]) for j in range(4)]
                    for tt_ in range(NCH):
                        pt, bpt = ps.get()
                        for j in range(4):
                            for kc in range(16):
                                P.mm(pt[:, j * 128:(j + 1) * 128], xT[:, kc, tt_ * 128:(tt_ + 1) * 128], wB[j][0][:, kc, :],
                                     start=(kc == 0), stop=(kc == 15), r=[wB[j][1], bxT], w=[bpt])
                        xg, bxg = ost.get()
                        t_, bt_ = ost.get()
                        P.cp("act", xg[:], pt[:, 0:512], r=[bpt], w=[bxg])
                        P.tt("pool", t_[:], xg[:], xg[:], ALU.mult, r=[bxg], w=[bt_])
                        P.ts("dve", t_[:], t_[:], 0.044715, 1.0, ALU.mult, ALU.add, r=[bt_], w=[bt_])
                        P.tt("dve", t_[:], t_[:], xg[:], ALU.mult, r=[bt_, bxg], w=[bt_])
                        P.act(t_[:], t_[:], AF.Sigmoid, r=[bt_], w=[bt_], scale=1.5957691216)
                        if cb == 0:
                            ub, bub = obf.get()
                            P.tt("dve", ub[:], t_[:], xg[:], ALU.mult, r=[bt_, bxg], w=[bub])
                            P.dma("sp", uscr[tt_ * 128:(tt_ + 1) * 128, :], ub[:], r=[bub], w=[bus])
                        else:
                            P.tt("dve", xg[:], t_[:], xg[:], ALU.mult, r=[bt_, bxg], w=[bxg])
                            vn, bvn = ost.get()
                            layer_norm_rows(xg, bxg, lnb, ba2, vn, bvn, 512)
                            vnb, bvnb = obf.get()
                            P.cp("act", vnb[:], vn[:], r=[bvn], w=[bvnb])
                            pm, bpm = ps.get()
                            for g in range(8):
                                P.mm(pm[:, g * 64:(g + 1) * 64], sgw[:, g, :], vnb[:, g * 64:(g + 1) * 64], r=[ba2, bvnb], w=[bpm])
                            ub, bub = obf.get()
                            P.dma("sp", ub[:], uscr[tt_ * 128:(tt_ + 1) * 128, :], r=[bus], w=[bub])
                            mx, bmx = ost.get()
                            P.tt("dve", mx[:, :].rearrange("p (g d) -> p g d", g=8), pm[:, 0:512].rearrange("p (g d) -> p g d", g=8),
                                 sgbt.unsqueeze(2).to_broadcast([128, 8, 64]), ALU.add, r=[bpm, bsmall], w=[bmx])
                            yb_, byb_ = obf.get()
                            P.tt("dve", yb_[:], mx[:], ub[:], ALU.mult, r=[bmx, bub], w=[byb_])
                            ptr, bptr = ps.get()
                            ptb = ptr[:, 0:256].bitcast(BF16)
                            for j in range(4):
                                P.tr(ptb[:, j * 128:(j + 1) * 128], yb_[:, j * 128:(j + 1) * 128], idb[:], r=[byb_, bc], w=[bptr])
                            yt_, byt_ = obf.get()
                            P.cp("act", yt_[:], ptb[:, 0:512], r=[bptr], w=[byt_])
                            for j in range(4):
                                P.dma("sp", ybrT[s, 1024 + j * 128:1024 + (j + 1) * 128, tt_ * 128:(tt_ + 1) * 128], yt_[:, j * 128:(j + 1) * 128],
                                      r=[byt_], w=[bybr[s][8 + j]])
                fence(a2bufs)
                qh = freeA[:, 0:T // 2].bitcast(BF16)
                kh = freeA[:, T // 2:T].bitcast(BF16)
                NV = NR * 2 * 65
                v1 = freeA[0:64, T:T + (NV + 1) // 2].bitcast(BF16)[:, 0:NV].rearrange("p (r h d) -> p r h d", r=NR, h=2)
                o_tb = T + (NV + 1) // 2 + 2
                tbias = freeA[0:64, o_tb:o_tb + 2 * 15 * 64].rearrange("p (h r j) -> p h r j", h=2, r=15)
                o_nm = o_tb + 2 * 15 * 64
                nmk = freeA[0:64, o_nm:o_nm + 64]
                o_yc = o_nm + 64
                ych = freeA[:, o_yc:o_yc + T // 2].bitcast(BF16)
                assert o_yc + T // 2 <= A2N
                P.dma("sp", nmk, nmask_d, w=[ba2])
                scale = 64 ** -0.5
                for hp in range(4):
                    P.dma("sp", qh, qkT[hp * 128:(hp + 1) * 128, :], r=[bqk], w=[ba2])
                    P.dma("sp", kh, qkT[512 + hp * 128:512 + (hp + 1) * 128, :], r=[bqk], w=[ba2])
                    P.dma("sp", v1, vrow[:, :, 2 * hp:2 * hp + 2, :], r=[bvr], w=[ba2])
                    P.dma("sp", tbias, rpbT_d[l, 2 * hp:2 * hp + 2].rearrange("h c r j -> c h r j"), w=[ba2])
                    P.tt("dve", tbias, tbias, nmk.unsqueeze(1).unsqueeze(1).to_broadcast([64, 2, 15, 64]), ALU.add, r=[ba2], w=[ba2])
                    for i0 in range(0, NR, 8):
                        ptr, bptr = psx.get()
                        ptb = ptr[:, 0:256].bitcast(BF16)
                        for ii in range(8):
                            i = i0 + ii
                            r0 = min(max(i - 4, 0), NR - 8)
                            ro0 = r0 - i + 7
                            pe_, bpe_ = ost.get()
                            pexp = pe_[0:64, :].bitcast(BF16).rearrange("p (h r j) -> p h r j", h=2, r=8)
                            for h in range(2):
                                hs = slice(h * 64, (h + 1) * 64)
                                psc, bpsc = ps.get()
                                for rr_ in range(8):
                                    kr = r0 + rr_
                                    P.mm(psc[0:64, rr_ * 64:(rr_ + 1) * 64], kh[hs, kr * 64:(kr + 1) * 64], qh[hs, i * 64:(i + 1) * 64], r=[ba2], w=[bpsc])
                                sc, bsc = ost.get()
                                P.stt(sc[0:64, :].rearrange("p (r j) -> p r j", r=8), psc[0:64, 0:512].rearrange("p (r j) -> p r j", r=8), scale,
                                      tbias[:, h, ro0:ro0 + 8, :], ALU.mult, ALU.add, r=[bpsc, ba2], w=[bsc])
                                P.act(pexp[:, h, :, :], sc[0:64, :].rearrange("p (r j) -> p r j", r=8), AF.Exp, r=[bsc], w=[bpe_])
                            po, bpo = ps.get()
                            for h in range(2):
                                for rr_ in range(8):
                                    P.mm(po[0:64, h * 66:h * 66 + 65], pexp[:, h, rr_, :], v1[:, r0 + rr_, h, :], start=(rr_ == 0), stop=(rr_ == 7), r=[bpe_, ba2], w=[bpo])
                            rs_, brs_ = cpool.get()
                            P.recip(rs_[0:64, 0:2], po[0:64, 0:132].rearrange("p (h d) -> p h d", h=2)[:, :, 64], r=[bpo], w=[brs_])
                            yr, byr = obf.get()
                            for h in range(2):
                                P.ts("dve", yr[0:64, h * 64:(h + 1) * 64], po[0:64, h * 66:h * 66 + 64], rs_[0:64, h:h + 1], None, ALU.mult, r=[bpo, brs_], w=[byr])
                            P.tr(ptb[:, ii * 64:(ii + 1) * 64], yr[0:64, 0:128], idb[0:64, 0:64], r=[byr, bc], w=[bptr])
                        P.cp("act", ych[:, i0 * 64:(i0 + 8) * 64], ptb[:, 0:512], r=[bptr], w=[ba2])
                    P.dma("sp", ybrT[s, 1536 + hp * 128:1536 + (hp + 1) * 128, :], ych, r=[ba2], w=[bybr[s][12 + hp]])

                fence([bxT] + arena_bufs)
                fence(a2bufs)
                for hp in range(8):
                    P.dma("sp", rr, rkv[s, hp * 128:(hp + 1) * 128, :], r=[brkv[s][hp]], w=[brr])
                    P.dma("sp", kk_, rkv[s, (8 + hp) * 128:(9 + hp) * 128, :], r=[brkv[s][8 + hp]], w=[bkk_])
                    P.dma("sp", vv, rkv[s, (16 + hp) * 128:(17 + hp) * 128, :], r=[brkv[s][16 + hp]], w=[bvv])
                    scan_pair(P, nc, K, T, hp, rr, kk_, vv, yy, bpre, (brr, bkk_, bvv, byy, bbpre), ps)
                    hcol = slice(hp * 128, (hp + 1) * 128)
                    for tb in range(0, T, 512):
                        w = min(512, T - tb)
                        bs = slice(tb, tb + w)
                        pm, bpm = ps.get()
                        P.mm(pm[:, 0:w], bones, yy[:, bs], r=[byy, bc], w=[bpm])
                        o1, bo1 = ost.get()
                        P.stt(o1[:, 0:w], pm[:, 0:w], -1.0 / 64, yy[:, bs], ALU.mult, ALU.add, r=[bpm, byy], w=[bo1])
                        o2, bo2 = ost.get()
                        P.tt("pool", o2[:, 0:w], o1[:, 0:w], o1[:, 0:w], ALU.mult, r=[bo1], w=[bo2])
                        pv, bpv = ps.get()
                        P.mm(pv[:, 0:w], bones, o2[:, 0:w], r=[bo2, bc], w=[bpv])
                        P.ts("dve", o2[:, 0:w], pv[:, 0:w], 1.0 / 64, GN_EPS, ALU.mult, ALU.add, r=[bpv], w=[bo2])
                        P.act(o2[:, 0:w], o2[:, 0:w], AF.Sqrt, r=[bo2], w=[bo2])
                        P.recip(o2[:, 0:w], o2[:, 0:w], r=[bo2], w=[bo2])
                        P.tt("dve", o1[:, 0:w], o1[:, 0:w], o2[:, 0:w], ALU.mult, r=[bo1, bo2], w=[bo1])
                        P.ts("dve", o1[:, 0:w], o1[:, 0:w], pcs[:, hp, 6:7], pcs[:, hp, 7:8], ALU.mult, ALU.add, r=[bo1, bc], w=[bo1])
                        pb_, bpb_ = ps.get()
                        P.mm(pb_[:, 0:w], bones, bpre[:, bs], r=[bbpre, bc], w=[bpb_])
                        P.tt("dve", o2[:, 0:w], pb_[:, 0:w], vv[:, bs], ALU.mult, r=[bpb_, bvv], w=[bo2])
                        P.tt("pool", o1[:, 0:w], o1[:, 0:w], o2[:, 0:w], ALU.add, r=[bo1, bo2], w=[bo1])
                        pg, bpg = ps.get()
                        P.mm(pg[:, 0:w], g2b[:, hcol], sgd[:, bs], start=True, stop=False, r=[bc, bsgd], w=[bpg])
                        P.mm(pg[:, 0:w], g2b2[:, hcol], sgd2[:, bs], start=False, stop=True, r=[bc, bsgd2], w=[bpg])
                        yab, byab = obf.get()
                        P.tt("dve", yab[:, 0:w], pg[:, 0:w], o1[:, 0:w], ALU.mult, r=[bpg, bo1], w=[byab])
                        P.dma("sp", ybrT[s, hcol, bs], yab[:, 0:w], r=[byab], w=[bybr[s][hp]])

                fence([bxT] + arena_bufs)
                fence(a2bufs)
                for kc in range(16):
                    P.dma("sp", xT[:, kc, :], ybrT[s, kc * 128:(kc + 1) * 128, :], r=[bybr[s][kc]], w=[bxT])
                gsb = free3[:, 0:3 * T // 2].bitcast(BF16).rearrange("p (g t) -> p g t", g=3)
                mo = free3[:, 3 * T // 2:2 * T].bitcast(BF16)
                for oc in range(16):
                    ocs = slice(oc * 128, (oc + 1) * 128)
                    ws, bws = wst.get()
                    wb, bwb = wbf.get()
                    P.dma("sp", ws[:, 0:8, :], pa_F[:, ocs].rearrange("(kc p) n -> p kc n", p=128), r=[bwfull[1]], w=[bws])
                    P.dma("sp", ws[:, 8:12, :], pb_F[:, ocs].rearrange("(kc p) n -> p kc n", p=128), r=[bwfull[2]], w=[bws])
                    P.dma("sp", ws[:, 12:16, :], pc_F[:, ocs].rearrange("(kc p) n -> p kc n", p=128), r=[bwfull[3]], w=[bws])
                    P.cp("pool", wb[:], ws[:], r=[bws], w=[bwb])
                    for g in range(3):
                        P.dma("sp", gsb[:, g, :], gT[s, g * 2048 + oc * 128:g * 2048 + (oc + 1) * 128, :], r=[bgT[s]], w=[ba2])
                    for tb in range(0, T, 512):
                        w = min(512, T - tb)
                        bs = slice(tb, tb + w)
                        acc, bacc = ost.get()
                        for g, (k0, k1) in enumerate(((0, 8), (8, 12), (12, 16))):
                            pt, bpt = ps.get()
                            for kc in range(k0, k1):
                                P.mm(pt[:, 0:w], wb[:, kc, :], xT[:, kc, bs], start=(kc == k0), stop=(kc == k1 - 1), r=[bwb, bxT], w=[bpt])
                            if g == 0:
                                P.tt("dve", acc[:, 0:w], pt[:, 0:w], gsb[:, 0, bs], ALU.mult, r=[bpt, ba2], w=[bacc])
                            else:
                                t_, bt_ = ost.get()
                                P.tt("dve", t_[:, 0:w], pt[:, 0:w], gsb[:, g, bs], ALU.mult, r=[bpt, ba2], w=[bt_])
                                if g == 1:
                                    P.tt("pool", acc[:, 0:w], acc[:, 0:w], t_[:, 0:w], ALU.add, r=[bacc, bt_], w=[bacc])
                                else:
                                    P.tt("pool", mo[:, bs], acc[:, 0:w], t_[:, 0:w], ALU.add, r=[bacc, bt_], w=[ba2])
                    P.dma("sp", mT[ocs, :], mo, r=[ba2], w=[bmT])
                fence([bxT] + arena_bufs)
                for kc in range(16):
                    P.dma("sp", xT[:, kc, :], mT[kc * 128:(kc + 1) * 128, :], r=[bmT], w=[bxT])
                for cb in range(4):
                    wo = [load_wblock(wout_F[:, cb * 512 + j * 128:cb * 512 + (j + 1) * 128], 128, r=[bwfull[4]]) for j in range(4)]
                    for tt_ in range(NCH):
                        pt, bpt = ps.get()
                        for j in range(4):
                            for kc in range(16):
                                P.mm(pt[:, j * 128:(j + 1) * 128], xT[:, kc, tt_ * 128:(tt_ + 1) * 128], wo[j][0][:, kc, :],
                                     start=(kc == 0), stop=(kc == 15), r=[wo[j][1], bxT], w=[bpt])
                        xr, bxr = ost.get()
                        rows = slice(s * T + tt_ * 128, s * T + (tt_ + 1) * 128)
                        P.dma("sp", xr[:], src[rows, cb * 512:(cb + 1) * 512], r=[bsrc], w=[bxr])
                        P.stt(xr[:], xr[:], float(ALPHA), pt[:, 0:512], ALU.mult, ALU.add, r=[bxr, bpt], w=[bxr])
                        P.dma("sp", hres[tt_ * 128:(tt_ + 1) * 128, cb * 512:(cb + 1) * 512], xr[:], r=[bxr], w=[bhres])
                fence(a2bufs)
                fence([bxT] + arena_bufs)
                gb1 = freeA[:, 0:2 * D_MODEL].rearrange("p (a c) -> p a c", a=2)
                P.dma("sp", gb1[:, 0, :], ln1_d[l, 0:1, :].to_broadcast([128, D_MODEL]), w=[ba2])
                P.dma("sp", gb1[:, 1, :], ln1_d[l, 1:2, :].to_broadcast([128, D_MODEL]), w=[ba2])
                wrf = freeA[:, 2 * D_MODEL:2 * D_MODEL + 256].rearrange("p (k e) -> p k e", k=16)
                wrb = small[:, 0:64]
                P.dma("sp", wrf, wr_d[l].rearrange("(kc p) e -> p kc e", p=128), w=[ba2])
                wrbb, bwrbb = wbf.get()
                wrb16 = wrbb[:, :, 0:16]
                P.cp("dve", wrb16, wrf, r=[ba2], w=[bwrbb])
                x1t = freeA[:, 2 * D_MODEL + 256:2 * D_MODEL + 256 + D_MODEL]
                bx1t = Buf("x1t")
                assert 2 * D_MODEL + 256 + D_MODEL <= A2N
                for tt_ in range(NCH):
                    xl, bxl = xld.get()
                    rows = slice(s * T + tt_ * 128, s * T + (tt_ + 1) * 128)
                    P.dma("sp", xl[:], hres[tt_ * 128:(tt_ + 1) * 128, :], r=[bhres], w=[bxl])
                    layer_norm_rows(xl, bxl, gb1, ba2, x1t, bx1t, D_MODEL)
                    P.dma("sp", x1[rows, :], x1t, r=[bx1t], w=[bx1])
                    for kq in range(4):
                        pt, bpt = ps.get()
                        for j in range(4):
                            kc = kq * 4 + j
                            P.tr(pt[:, j * 128:(j + 1) * 128], x1t[:, kc * 128:(kc + 1) * 128], idf, r=[bx1t, bc], w=[bpt])
                        eng = "act" if kq % 2 == 0 else "dve"
                        P.cp(eng, xT[:, kq * 4:(kq + 1) * 4, tt_ * 128:(tt_ + 1) * 128],
                             pt[:, 0:512].rearrange("p (j t) -> p j t", j=4), r=[bpt], w=[bxT])
                    pr, bpr = ps.get()
                    for kc in range(16):
                        P.mm(pr[:, 0:16], xT[:, kc, tt_ * 128:(tt_ + 1) * 128], wrb16[:, kc, :], start=(kc == 0), stop=(kc == 15), r=[bxT, bwrbb], w=[bpr])
                    lg, blg = cpool.get()
                    P.add("dve", lambda e, lg=lg, pr=pr: e.reduce_max(out=lg[:, 16:17], in_=pr[:, 0:16], axis=mybir.AxisListType.X), reads=[bpr], writes=[blg])
                    P.ts("dve", lg[:, 17:18], lg[:, 16:17], -1.0, None, ALU.mult, r=[blg], w=[blg])
                    P.act(lg[:, 0:16], pr[:, 0:16], AF.Exp, r=[bpr, blg], w=[blg], bias=lg[:, 17:18])
                    P.add("dve", lambda e, lg=lg: e.reduce_sum(out=lg[:, 18:19], in_=lg[:, 0:16], axis=mybir.AxisListType.X), reads=[blg], writes=[blg])
                    P.recip(lg[:, 18:19], lg[:, 18:19], r=[blg], w=[blg])
                    P.ts("dve", lg[:, 0:16], lg[:, 0:16], lg[:, 18:19], None, ALU.mult, r=[blg], w=[blg])
                    P.dma("sp", aff[rows, :], lg[:, 0:16], r=[blg], w=[baff])
                for kc in range(16):
                    P.dma("sp", gT[s, kc * 128:(kc + 1) * 128, :], xT[:, kc, :], r=[bxT, bgT[s]], w=[bgT[s]])

            P.add("pool", lambda e: e.collective_compute("AllGather", ALU.bypass, replica_groups=[list(range(NCORES))],
                                                         ins=[aff[:, :]], outs=[aff_all[:, :]]), reads=[baff], writes=[baffall])
            fence(a2bufs)
            RT = T // 128
            maxR = max(len(g) for g in groups) * RT
            assert 2 * maxR * 16 + 96 <= A2N
            afg = freeA[:, 0:maxR * 16].rearrange("p (r e) -> p r e", e=16)
            cmpb = freeA[:, maxR * 16:2 * maxR * 16].rearrange("p (r e) -> p r e", e=16)
            o_ = 2 * maxR * 16
            lo = freeA[:, o_:o_ + 16]
            hi = freeA[:, o_ + 16:o_ + 32]
            mid = freeA[:, o_ + 32:o_ + 48]
            cnt = freeA[:, o_ + 48:o_ + 64]
            ge = freeA[:, o_ + 64:o_ + 80]
            dlt = freeA[:, o_ + 80:o_ + 96]
            for gi, g in enumerate(groups):
                n_g = len(g) * T
                cap = 2 * n_g // NEXP
                R = len(g) * RT
                for bi, (c, s_) in enumerate(g):
                    base = (c * NS + s_) * T
                    P.dma("sp", afg[:, bi * RT:(bi + 1) * RT, :], aff_all[base:base + T, :].rearrange("(r p) e -> p r e", p=128), r=[baffall], w=[ba2])
                P.memset("pool", lo, 0.0, w=[ba2])
                P.memset("pool", hi, 1.0, w=[ba2])
                for it in range(NBIS):
                    P.tt("dve", mid, lo, hi, ALU.add, r=[ba2], w=[ba2])
                    P.ts("dve", mid, mid, 0.5, None, ALU.mult, r=[ba2], w=[ba2])
                    P.tt("dve", cmpb[:, 0:R, :], afg[:, 0:R, :], mid.unsqueeze(1).to_broadcast([128, R, 16]), ALU.is_ge, r=[ba2], w=[ba2])
                    P.add("dve", lambda e, R=R: e.reduce_sum(out=cnt, in_=cmpb[:, 0:R, :].rearrange("p r e -> p e r"), axis=mybir.AxisListType.X), reads=[ba2], writes=[ba2])
                    pc_, bpc_ = ps.get()
                    P.mm(pc_[:, 0:16], onesf[:], cnt, r=[ba2, bc], w=[bpc_])
                    P.ts("dve", ge, pc_[:, 0:16], float(cap) - 0.5, None, ALU.is_ge, r=[bpc_], w=[ba2])
                    P.tt("dve", dlt, mid, lo, ALU.subtract, r=[ba2], w=[ba2])
                    P.tt("dve", dlt, dlt, ge, ALU.mult, r=[ba2], w=[ba2])
                    P.tt("dve", lo, lo, dlt, ALU.add, r=[ba2], w=[ba2])
                    P.tt("dve", dlt, hi, mid, ALU.subtract, r=[ba2], w=[ba2])
                    P.tt("dve", dlt, dlt, ge, ALU.mult, r=[ba2], w=[ba2])
                    P.tt("dve", hi, mid, dlt, ALU.add, r=[ba2], w=[ba2])
                P.cp("dve", thr[:, gi, :], lo, r=[ba2], w=[bthr])
            if debug:
                P.dma("sp", dbg["thr"][:, gi, :], thr[:, gi, :], r=[bthr], w=[Buf("dthr")])
                if gi == 0:
                    xl, bxl = xld.get()
                    P.dma("sp", xl[:, 0:NT * 16 // 128], aff.rearrange("(p r) e -> p (r e)", p=128), r=[baff], w=[bxl])
                    P.dma("sp", dbg["aff"].rearrange("(p r) e -> p (r e)", p=128), xl[:, 0:NT * 16 // 128], r=[bxl], w=[Buf("daff")])

            for s in range(NS):
                gi = slot_group[s]
                fence([bxT] + arena_bufs)
                fence(a2bufs)
                for kc in range(16):
                    P.dma("sp", xT[:, kc, :], gT[s, kc * 128:(kc + 1) * 128, :], r=[bgT[s]], w=[bxT])
                yacc = freeA[:, 0:4 * D_MODEL].rearrange("p (t c) -> p t c", t=4)
                hT = freeA[:, 4 * D_MODEL:4 * D_MODEL + 2048].bitcast(BF16).rearrange("p (h t) -> p h t", h=8)
                gb2 = None
                assert 4 * D_MODEL + 2048 <= A2N
                mgt = small[:, 0:64].rearrange("p (t e) -> p t e", t=4)
                for tb in range(0, T, 512):
                    ntile = min(4, (T - tb) // 128)
                    for t4 in range(ntile):
                        rows = slice(s * T + tb + t4 * 128, s * T + tb + (t4 + 1) * 128)
                        af, baf = cpool.get()
                        P.dma("sp", af[:, 0:16], aff[rows, :], r=[baff], w=[baf])
                        P.tt("dve", af[:, 16:32], af[:, 0:16], thr[:, gi, :], ALU.is_ge, r=[baf, bthr], w=[baf])
                        P.tt("dve", mgt[:, t4, :], af[:, 16:32], af[:, 0:16], ALU.mult, r=[baf], w=[bsmall])
                    for e in range(NEXP):
                        egb = eall[0].rearrange("(b p f) -> b p f", p=128, f=2048)
                        eub = eall[1].rearrange("(b p f) -> b p f", p=128, f=2048)
                        edv = eall[2].rearrange("(e f c) -> e f c", e=NEXP, f=EH)
                        for hb in range(8):
                            wg, bwg = wbf.get()
                            wu, bwu = wbf.get()
                            P.dma("sp", wg[:, :, :].rearrange("p a b -> p (a b)"), egb[e * 8 + hb], r=[beall[0]], w=[bwg])
                            P.dma("sp", wu[:, :, :].rearrange("p a b -> p (a b)"), eub[e * 8 + hb], r=[beall[1]], w=[bwu])
                            pg_, bpg_ = ps.get()
                            pu_, bpu_ = ps.get()
                            W_ = ntile * 128
                            for kc in range(16):
                                P.mm(pg_[:, 0:W_], wg[:, kc, :], xT[:, kc, tb:tb + W_], start=(kc == 0), stop=(kc == 15), r=[bwg, bxT], w=[bpg_])
                            for kc in range(16):
                                P.mm(pu_[:, 0:W_], wu[:, kc, :], xT[:, kc, tb:tb + W_], start=(kc == 0), stop=(kc == 15), r=[bwu, bxT], w=[bpu_])
                            sg_, bsg_ = ost.get()
                            P.act(sg_[:, 0:W_], pg_[:, 0:W_], AF.Silu, r=[bpg_], w=[bsg_])
                            P.tt("dve", hT[:, hb, 0:W_], sg_[:, 0:W_], pu_[:, 0:W_], ALU.mult, r=[bsg_, bpu_], w=[ba2])
                        for cb in range(4):
                            wd0, bwd0 = wbf.get()
                            wd1, bwd1 = wbf.get()
                            wdv = [wd0[:, :, :].rearrange("p a b -> p (a b)").rearrange("p (h c) -> p h c", h=4),
                                   wd1[:, :, :].rearrange("p a b -> p (a b)").rearrange("p (h c) -> p h c", h=4)]
                            P.dma("sp", wdv[0], edv[e, 0:512, cb * 512:(cb + 1) * 512].rearrange("(h p) c -> p h c", p=128), r=[beall[2]], w=[bwd0])
                            P.dma("sp", wdv[1], edv[e, 512:1024, cb * 512:(cb + 1) * 512].rearrange("(h p) c -> p h c", p=128), r=[beall[2]], w=[bwd1])
                            for t4 in range(ntile):
                                py_, bpy_ = ps.get()
                                for hb in range(8):
                                    P.mm(py_[:, 0:512], hT[:, hb, t4 * 128:(t4 + 1) * 128], wdv[hb // 4][:, hb % 4, :], start=(hb == 0), stop=(hb == 7),
                                         r=[ba2, bwd0, bwd1], w=[bpy_])
                                ysl = yacc[:, t4, cb * 512:(cb + 1) * 512]
                                if e == 0:
                                    P.ts("dve", ysl, py_[:, 0:512], mgt[:, t4, e:e + 1], None, ALU.mult, r=[bpy_, bsmall], w=[ba2])
                                else:
                                    P.stt(ysl, py_[:, 0:512], mgt[:, t4, e:e + 1], ysl, ALU.mult, ALU.add, r=[bpy_, bsmall, ba2], w=[ba2])
                    for t4 in range(ntile):
                        rows = slice(s * T + tb + t4 * 128, s * T + tb + (t4 + 1) * 128)
                        xl, bxl = xld.get()
                        P.dma("sp", xl[:], x1[rows, :], r=[bx1], w=[bxl])
                        P.stt(xl[:], xl[:], float(ALPHA), yacc[:, t4, :], ALU.mult, ALU.add, r=[bxl, ba2], w=[bxl])
                        st_, bst_ = cpool.get()
                        for j in range(4):
                            P.add("dve", lambda e_, j=j, st_=st_, xl=xl: e_.bn_stats(out=st_[:, j * 6:(j + 1) * 6], in_=xl[:, j * 512:(j + 1) * 512]), reads=[bxl], writes=[bst_])
                        mv, bmv = cpool.get()
                        P.add("dve", lambda e_, mv=mv, st_=st_: e_.bn_aggr(out=mv[:, 0:2], in_=st_[:, 0:24]), reads=[bst_], writes=[bmv])
                        P.ts("dve", mv[:, 2:3], mv[:, 1:2], LN_EPS, None, ALU.add, r=[bmv], w=[bmv])
                        P.act(mv[:, 2:3], mv[:, 2:3], AF.Sqrt, r=[bmv], w=[bmv])
                        P.recip(mv[:, 2:3], mv[:, 2:3], r=[bmv], w=[bmv])
                        P.ts("dve", xl[:], xl[:], mv[:, 0:1], mv[:, 2:3], ALU.subtract, ALU.mult, r=[bxl, bmv], w=[bxl])
                        for j in range(4):
                            gch, bgch = ost.get()
                            bch, bbch = ost.get()
                            P.dma("sp", gch[:], ln2_d[l, 0:1, j * 512:(j + 1) * 512].to_broadcast([128, 512]), w=[bgch])
                            P.dma("sp", bch[:], ln2_d[l, 1:2, j * 512:(j + 1) * 512].to_broadcast([128, 512]), w=[bbch])
                            P.tt("pool", xl[:, j * 512:(j + 1) * 512], xl[:, j * 512:(j + 1) * 512], gch[:], ALU.mult, r=[bxl, bgch], w=[bxl])
                            P.tt("pool", xl[:, j * 512:(j + 1) * 512], xl[:, j * 512:(j + 1) * 512], bch[:], ALU.add, r=[bxl, bbch], w=[bxl])
                        P.dma("sp", dst[rows, :], xl[:], r=[bxl], w=[bdst])
        P.emit()
        print("ops", P.nops)
    return nc


def prep_consts(inp, L):
    f = np.float32
    mu = np.zeros((L, 2, 28 * 128), f)
    mu[:, 0, :A_COLS] = inp["mu_prev"][:L]
    mu[:, 1, :A_COLS] = inp["mu_next"][:L]
    mu = np.ascontiguousarray(mu.reshape(L, 2, 28, 128).transpose(0, 3, 1, 2))
    pcs = np.zeros((L, 10, 8, 128), f)

    def chp(a):
        return a.reshape(L, 8, 128)
    pcs[:, 0] = chp(inp["k_k"][:L])
    pcs[:, 1] = chp(inp["k_a"][:L])
    pcs[:, 3] = chp(inp["r_k"][:L].reshape(L, A_WIDTH))
    pcs[:, 4] = chp(inp["iclr_a0"][:L, 0])
    pcs[:, 5] = chp(inp["iclr_a0"][:L, 1])
    pcs[:, 6] = chp(inp["gn_g"][:L])
    pcs[:, 7] = chp(inp["gn_b"][:L])
    pcs[:, 8] = chp(inp["decay_w0"][:L, 0])
    pcs[:, 9] = chp(inp["decay_w0"][:L, 1])
    pcs = np.ascontiguousarray(pcs.transpose(0, 3, 2, 1))
    c = np.arange(64)[:, None]
    j = np.arange(64)[None, :]
    co = np.clip(c - j + 15, 0, 30)
    rpbT = inp["rpb"][:L][:, :, :, co]
    rpbT = np.ascontiguousarray(rpbT.transpose(0, 1, 3, 2, 4))
    c0 = np.clip(np.arange(64) - 8, 0, 48)[None, :]
    nmask = np.where((c >= c0) & (c < c0 + 16), 0.0, -30000.0).astype(f)
    d = dict(
        mu=mu, pcs=pcs,
        w2=np.ascontiguousarray(inp["decay_w2"][:L].reshape(L, 128, A_WIDTH)),
        a2=np.ascontiguousarray(inp["iclr_a2"][:L].reshape(L, 128, A_WIDTH)),
        g2=np.ascontiguousarray(inp["gate_g2"][:L]),
        cmat=host_consts(),
        sgln=np.ascontiguousarray(np.stack([inp["sg_ln_g"][:L], inp["sg_ln_b"][:L]], 1)),
        sgwT=np.ascontiguousarray(inp["sg_w"][:L].transpose(0, 3, 1, 2)),
        sgb=np.ascontiguousarray(inp["sg_b"][:L].transpose(0, 2, 1)),
        rpbT=rpbT, nmask=nmask,
        ln1=np.ascontiguousarray(np.stack([inp["ln_mix_g"][:L], inp["ln_mix_b"][:L]], 1)),
        ln2=np.ascontiguousarray(np.stack([inp["ln_ffn_g"][:L], inp["ln_ffn_b"][:L]], 1)),
        w_router=np.ascontiguousarray(inp["w_router"][:L]),
    )
    return d


def expert_shard(inp, L, c):
    def rows(a):
        n = a.shape[1] // NCORES
        return np.ascontiguousarray(a[:L, c * n:(c + 1) * n])
    return dict(w_in=rows(inp["w_in"]), p_a=rows(inp["p_a"]), p_b=rows(inp["p_b"]), p_c=rows(inp["p_c"]), w_out=rows(inp["w_out"]),
                eg=np.ascontiguousarray(inp["e_gate"][:L, 2 * c:2 * c + 2]),
                eu=np.ascontiguousarray(inp["e_up"][:L, 2 * c:2 * c + 2]),
                ed=np.ascontiguousarray(inp["e_down"][:L, 2 * c:2 * c + 2]))


_CACHE = {}


def kernel(**inputs):
    L = DEPTH
    T = 2048
    inp = {k: np.asarray(v) for k, v in inputs.items()}
    xp = inp["x_prompt"].astype(np.float32, copy=False)
    xs = inp["x_sample"].astype(np.float32, copy=False)
    consts = prep_consts(inp, L)
    in_maps = []
    for c in range(NCORES):
        s0 = xp[c] if c < 4 else xs[2 * c]
        m = dict(x=np.ascontiguousarray(np.concatenate([s0, xs[2 * c], xs[2 * c + 1]], 0)))
        m.update(consts)
        m.update(expert_shard(inp, L, c))
        in_maps.append(m)
    if "nc" not in _CACHE:
        _CACHE["nc"] = build_program(L=L, NS=NSLOT, T=T)
    res = run_bass_kernel_spmd(_CACHE["nc"], in_maps, core_ids=list(range(NCORES)))
    yp = np.zeros_like(xp)
    ys = np.zeros_like(xs)
    for c in range(NCORES):
        y = res.results[c]["y"].reshape(NSLOT, T, D_MODEL)
        if c < 4:
            yp[c] = y[0]
        ys[2 * c] = y[1]
        ys[2 * c + 1] = y[2]
    return (yp, ys)
```
